# Optimizing a Trainium2 kernel written in Bass

```python
import math
import jax, jax.numpy as jnp
from jax import lax
import numpy as np

D_MODEL = 1024
BATCH = 8
SEQ = 4096
DEPTH = 4

N_MIXERS = 2
D_FF = 4 * D_MODEL
RMS_EPS = 1e-6
NEG_INF = -1e30

GDN_HEADS = D_MODEL // 128
GDN_DK = 128
GDN_DV = 128
GDN_CONV = 5
GDN_CHUNK = 64
GDN_QKV = GDN_HEADS * (2 * GDN_DK + GDN_DV)
GDN_IN = GDN_QKV + GDN_HEADS * GDN_DV + 4 * GDN_HEADS

DSWA_CONFIGS = ((128, 1), (512, 4), (2048, 16))
DSWA_HEADS_PER_GROUP = 6
DSWA_HEAD_DIM = 64
DSWA_HEADS = len(DSWA_CONFIGS) * DSWA_HEADS_PER_GROUP
DSWA_WIDTH = DSWA_HEADS * DSWA_HEAD_DIM

REL_BUCKETS = 32
REL_MAX_DIST = 1024

N_LAYERS_A = (DEPTH + 1) // 2
N_LAYERS_B = DEPTH // 2

kernel_name = 'hybrid_gdn_dilated_swa_encoder'


def rmsnorm(x, g):
    xf = x.astype(jnp.float32)
    y = xf * lax.rsqrt(jnp.mean(xf * xf, axis=-1, keepdims=True) + RMS_EPS)
    return (y * g.astype(jnp.float32)).astype(x.dtype)


def l2norm(t):
    return t * lax.rsqrt(jnp.sum(t * t, axis=-1, keepdims=True) + 1e-6)


def rel_bucket(rel):
    nb = REL_BUCKETS // 2
    max_exact = nb // 2
    ret = jnp.where(rel > 0, nb, 0)
    n = jnp.abs(rel)
    nf = jnp.maximum(n, 1).astype(jnp.float32)
    large = max_exact + (jnp.log(nf / max_exact) / math.log(REL_MAX_DIST / max_exact)
                         * (nb - max_exact)).astype(jnp.int32)
    large = jnp.minimum(large, nb - 1)
    return ret + jnp.where(n < max_exact, n, large)


def gated_delta_chunked(q, k, v, g, beta):
    B, H, S, DK = q.shape
    DV = v.shape[-1]
    C = GDN_CHUNK
    nc = S // C
    q = q.reshape(B, H, nc, C, DK)
    k = k.reshape(B, H, nc, C, DK)
    v = v.reshape(B, H, nc, C, DV)
    g = g.reshape(B, H, nc, C)
    beta = beta.reshape(B, H, nc, C)
    G = jnp.cumsum(g, axis=-1)
    incl = jnp.tril(jnp.ones((C, C), bool))
    strict = jnp.tril(jnp.ones((C, C), bool), -1)
    diff = jnp.where(incl, G[..., :, None] - G[..., None, :], 0.0)
    decay = jnp.where(incl, jnp.exp(diff), 0.0)
    kb = k * beta[..., None]
    a = jnp.where(strict, jnp.einsum('bhnid,bhnjd->bhnij', kb, k) * decay, 0.0)
    rhs = jnp.concatenate([v * beta[..., None], kb * jnp.exp(G)[..., None]], axis=-1)
    sol = lax.linalg.triangular_solve(a, rhs, left_side=True, lower=True, unit_diagonal=True)
    u, w = sol[..., :DV], sol[..., DV:]
    intra = jnp.einsum('bhnid,bhnjd->bhnij', q, k) * decay
    g_last = G[..., -1]
    q_dec = q * jnp.exp(G)[..., None]
    k_dec = k * jnp.exp(g_last[..., None] - G)[..., None]
    xs = tuple(jnp.moveaxis(t, 2, 0) for t in (u, w, intra, q_dec, k_dec, g_last))

    def step(state, inp):
        u_c, w_c, intra_c, qd_c, kd_c, gl_c = inp
        v_new = u_c - jnp.einsum('bhck,bhkv->bhcv', w_c, state)
        o_c = (jnp.einsum('bhck,bhkv->bhcv', qd_c, state)
               + jnp.einsum('bhij,bhjv->bhiv', intra_c, v_new))
        state = (state * jnp.exp(gl_c)[..., None, None]
                 + jnp.einsum('bhck,bhcv->bhkv', kd_c, v_new))
        return state, o_c

    s0 = jnp.zeros((B, H, DK, DV), jnp.float32)
    _, o = lax.scan(step, s0, xs)
    return jnp.moveaxis(o, 0, 2).reshape(B, H, S, DV)


def gated_deltanet_bidir(h, w_in, conv_w, a_log, dt_bias, norm_w, w_out):
    B, S, _ = h.shape
    H, DK, DV = GDN_HEADS, GDN_DK, GDN_DV
    f32 = jnp.float32
    proj = h @ w_in
    qkv = proj[..., :GDN_QKV]
    z = proj[..., GDN_QKV:GDN_QKV + H * DV].reshape(B, S, H, DV)
    ab = proj[..., GDN_QKV + H * DV:]
    pad = GDN_CONV // 2
    qkv = jax.nn.silu(lax.conv_general_dilated(
        qkv, conv_w, window_strides=(1,), padding=[(pad, pad)],
        dimension_numbers=('NWC', 'WIO', 'NWC'), feature_group_count=GDN_QKV))
    qkv = qkv.astype(f32)
    q = qkv[..., :H * DK].reshape(B, S, H, DK)
    k = qkv[..., H * DK:2 * H * DK].reshape(B, S, H, DK)
    v = qkv[..., 2 * H * DK:].reshape(B, S, H, DV)
    q = l2norm(q) * (DK ** -0.5)
    k = l2norm(k)
    a = ab[..., :2 * H].reshape(B, S, 2, H).astype(f32)
    b = ab[..., 2 * H:].reshape(B, S, 2, H).astype(f32)
    g = -jnp.exp(a_log.astype(f32)) * jax.nn.softplus(a + dt_bias.astype(f32))
    beta = jax.nn.sigmoid(b)
    qh, kh, vh = (jnp.transpose(t, (0, 2, 1, 3)) for t in (q, k, v))
    gh = jnp.transpose(g, (2, 0, 3, 1))
    bh = jnp.transpose(beta, (2, 0, 3, 1))
    o_fwd = gated_delta_chunked(qh, kh, vh, gh[0], bh[0])
    flip = lambda t: jnp.flip(t, axis=2)
    o_bwd = flip(gated_delta_chunked(flip(qh), flip(kh), flip(vh), flip(gh[1]), flip(bh[1])))
    o = jnp.transpose(o_fwd + o_bwd, (0, 2, 1, 3))
    o = o * lax.rsqrt(jnp.mean(o * o, axis=-1, keepdims=True) + RMS_EPS)
    o = o * norm_w.astype(f32) * jax.nn.silu(z.astype(f32))
    return o.reshape(B, S, H * DV).astype(h.dtype) @ w_out


def banded_attention(q, k, v, bias, half):
    N, L, H, E = q.shape
    blk = half
    nb = -(-L // blk)
    lp = nb * blk
    qb = jnp.pad(q, ((0, 0), (0, lp - L), (0, 0), (0, 0))).reshape(N, nb, blk, H, E)

    def windows(t):
        tb = jnp.pad(t, ((0, 0), (blk, lp - L + blk), (0, 0), (0, 0))).reshape(N, nb + 2, blk, H, E)
        return jnp.concatenate([tb[:, :-2], tb[:, 1:-1], tb[:, 2:]], axis=2)

    kw, vw = windows(k), windows(v)
    s = jnp.einsum('nbqhe,nbkhe->nbhqk', qb, kw).astype(jnp.float32) * (E ** -0.5)
    s = s + bias[None, None]
    q_idx = jnp.arange(lp).reshape(nb, blk)
    k_idx = jnp.arange(nb)[:, None] * blk - blk + jnp.arange(3 * blk)[None, :]
    off = k_idx[:, None, :] - q_idx[:, :, None]
    valid = (jnp.abs(off) <= half) & (k_idx[:, None, :] >= 0) & (k_idx[:, None, :] < L)
    s = jnp.where(valid[None, :, None], s, NEG_INF)
    m = jnp.max(s, axis=-1, keepdims=True)
    lse = m + jnp.log(jnp.sum(jnp.exp(s - m), axis=-1, keepdims=True))
    p = jnp.exp(s - lse)
    o = jnp.einsum('nbhqk,nbkhe->nbqhe', p.astype(v.dtype), vw).reshape(N, lp, H, E)[:, :L]
    lse = jnp.transpose(lse[..., 0], (0, 1, 3, 2)).reshape(N, lp, H)[:, :L]
    return o, lse


def to_sub(t, d):
    B, S, Hh, E = t.shape
    return jnp.swapaxes(t.reshape(B, S // d, d, Hh, E), 1, 2).reshape(B * d, S // d, Hh, E)


def from_sub(t, d, B):
    L = t.shape[1]
    rest = t.shape[2:]
    return jnp.swapaxes(t.reshape(B, d, L, *rest), 1, 2).reshape(B, L * d, *rest)


def dilated_window_attention(h, w_in, w_out, rel_table):
    B, S, _ = h.shape
    Hg, E = DSWA_HEADS_PER_GROUP, DSWA_HEAD_DIM
    qkv = (h @ w_in).reshape(B, S, 3, DSWA_HEADS, E)
    q, k, v = qkv[:, :, 0], qkv[:, :, 1], qkv[:, :, 2]
    outs, lses = [], []
    for gi, (window, dil) in enumerate(DSWA_CONFIGS):
        half = window // (2 * dil)
        hs = slice(gi * Hg, (gi + 1) * Hg)
        rel = (jnp.arange(3 * half)[None, :] - half - jnp.arange(half)[:, None]) * dil
        bias = jnp.take(rel_table, rel_bucket(rel), axis=0)[..., hs]
        bias = jnp.transpose(bias, (2, 0, 1)).astype(jnp.float32)
        o, lse = banded_attention(to_sub(q[:, :, hs], dil), to_sub(k[:, :, hs], dil),
                                  to_sub(v[:, :, hs], dil), bias, half)
        outs.append(from_sub(o, dil, B))
        lses.append(from_sub(lse, dil, B))
    o = jnp.stack(outs, axis=2)
    alpha = jax.nn.softmax(jnp.stack(lses, axis=2), axis=2)
    o = (o * alpha[..., None].astype(o.dtype)).reshape(B, S, DSWA_WIDTH)
    return o @ w_out


def squared_relu_mlp(h, w1, w2):
    return jnp.square(jax.nn.relu(h @ w1)) @ w2


def setup_inputs(seed: int = 0) -> dict:
    key = jax.random.key(seed)
    ks = jax.random.split(key, 16)
    f32 = jnp.float32
    nrm = lambda kk, shape, scale: jax.random.normal(kk, shape, f32) * scale
    x = nrm(ks[0], (BATCH, SEQ, D_MODEL), 1.0)
    norm_mix = 1.0 + nrm(ks[1], (DEPTH, D_MODEL), 0.02)
    norm_mlp = 1.0 + nrm(ks[2], (DEPTH, D_MODEL), 0.02)
    norm_final = 1.0 + nrm(ks[3], (D_MODEL,), 0.02)
    rel_bias = nrm(ks[4], (REL_BUCKETS, DSWA_HEADS), 0.2)
    gdn_w_in = nrm(ks[5], (N_LAYERS_A, D_MODEL, GDN_IN), D_MODEL ** -0.5)
    gdn_conv_w = nrm(ks[6], (N_LAYERS_A, GDN_CONV, 1, GDN_QKV), GDN_CONV ** -0.5)
    gdn_a_log = jnp.log(jax.random.uniform(ks[7], (N_LAYERS_A, 2, GDN_HEADS), f32, 1.0, 16.0))
    dt = jnp.exp(jax.random.uniform(ks[8], (N_LAYERS_A, 2, GDN_HEADS), f32,
                                    math.log(1e-3), math.log(1e-1)))
    gdn_dt_bias = dt + jnp.log(-jnp.expm1(-dt))
    gdn_norm_w = 1.0 + nrm(ks[9], (N_LAYERS_A, GDN_DV), 0.02)
    gdn_w_out = nrm(ks[10], (N_LAYERS_A, GDN_HEADS * GDN_DV, D_MODEL), (GDN_HEADS * GDN_DV) ** -0.5)
    dswa_w_in = nrm(ks[11], (N_LAYERS_B, D_MODEL, 3 * DSWA_WIDTH), D_MODEL ** -0.5)
    dswa_w_out = nrm(ks[12], (N_LAYERS_B, DSWA_WIDTH, D_MODEL), DSWA_WIDTH ** -0.5)
    mlp_w1 = nrm(ks[13], (DEPTH, D_MODEL, D_FF), D_MODEL ** -0.5)
    mlp_w2 = nrm(ks[14], (DEPTH, D_FF, D_MODEL), D_FF ** -0.5)
    return {'x': x, 'norm_mix': norm_mix, 'norm_mlp': norm_mlp, 'norm_final': norm_final,
            'rel_bias': rel_bias, 'gdn_w_in': gdn_w_in, 'gdn_conv_w': gdn_conv_w,
            'gdn_a_log': gdn_a_log, 'gdn_dt_bias': gdn_dt_bias, 'gdn_norm_w': gdn_norm_w,
            'gdn_w_out': gdn_w_out, 'dswa_w_in': dswa_w_in, 'dswa_w_out': dswa_w_out,
            'mlp_w1': mlp_w1, 'mlp_w2': mlp_w2}


def reference(x, norm_mix, norm_mlp, norm_final, rel_bias, gdn_w_in, gdn_conv_w,
              gdn_a_log, gdn_dt_bias, gdn_norm_w, gdn_w_out, dswa_w_in, dswa_w_out,
              mlp_w1, mlp_w2):
    for i in range(DEPTH):
        h = rmsnorm(x, norm_mix[i])
        j = i // N_MIXERS
        if i % N_MIXERS == 0:
            y = gated_deltanet_bidir(h, gdn_w_in[j], gdn_conv_w[j], gdn_a_log[j],
                                     gdn_dt_bias[j], gdn_norm_w[j], gdn_w_out[j])
        else:
            y = dilated_window_attention(h, dswa_w_in[j], dswa_w_out[j], rel_bias)
        x = x + y
        h = rmsnorm(x, norm_mlp[i])
        x = x + squared_relu_mlp(h, mlp_w1[i], mlp_w2[i])
    return rmsnorm(x, norm_final)
```

```python
import contextlib
import math
import numpy as np
import ml_dtypes
import concourse.bass as bass
import concourse.mybir as mybir
from concourse.bass_utils import run_bass_kernel_spmd

F32 = mybir.dt.float32
BF16 = mybir.dt.bfloat16
AF = mybir.ActivationFunctionType
ALU = mybir.AluOpType
AX = mybir.AxisListType

S = 4096
D = 1024
DFF = 4096
NT = S // 128
EPS = 1e-6
NEG = -30000.0
DBG_STOP = 99
DBG_SUB = 99
DBG_NB = 99


class Sched:
    ENGS = ("pe", "act", "dve", "pool", "sp")
    NDMA = 8

    def __init__(self, nc, same_engine_sync=True):
        self.nc = nc
        self.ops = {e: [] for e in self.ENGS}
        self.state = {}
        self.same = same_engine_sync
        self.bar_start = {e: 0 for e in self.ENGS}
        self.pending = {}

    def barrier(self):
        lasts = set()
        for e in self.ENGS:
            ol = self.ops[e]
            for i in range(len(ol) - 1, -1, -1):
                if not ol[i]["dma"]:
                    lasts.add((e, i))
                    break
            for i in range(self.bar_start[e], len(ol)):
                if ol[i]["dma"]:
                    lasts.add((e, i))
            self.bar_start[e] = len(ol)
        self.pending = {e: set(lasts) for e in self.ENGS}

    def add(self, eng, fn, r=(), w=(), wp=(), dma=False):
        idx = len(self.ops[eng])
        deps = set(self.pending.pop(eng, ()))
        for k in r:
            st = self.state.get(k)
            if st is None:
                st = self.state[k] = {"w": [], "r": []}
            deps.update(st["w"])
        for k, partial in [(k, False) for k in w] + [(k, True) for k in wp]:
            st = self.state.get(k)
            if st is None:
                st = self.state[k] = {"w": [], "r": []}
            if st["r"]:
                deps.update(st["r"])
                deps.update(st["w"])
                st["w"] = []
                st["r"] = []
            elif not partial:
                deps.update(st["w"])
                st["w"] = []
        me = (eng, idx)
        for k in r:
            self.state[k]["r"].append(me)
        for k in list(w) + list(wp):
            self.state[k]["w"].append(me)
        best = {}
        keep = set()
        for (e2, i2) in deps:
            o2 = self.ops[e2][i2]
            if o2["dma"]:
                keep.add((e2, i2))
                continue
            if e2 == eng:
                if eng == "pe" or not self.same:
                    continue
            if e2 not in best or best[e2] < i2:
                best[e2] = i2
        for e2, i2 in best.items():
            keep.add((e2, i2))
        self.ops[eng].append({"fn": fn, "deps": keep, "dma": dma, "sig": False})

    def emit(self):
        nc = self.nc
        ops = self.ops
        for e in self.ENGS:
            for op in ops[e]:
                for (e2, i2) in op["deps"]:
                    ops[e2][i2]["sig"] = True
        for e in self.ENGS:
            cnt = 0
            nd = 0
            per = [0] * self.NDMA
            for op in ops[e]:
                if op["dma"]:
                    j = nd % self.NDMA
                    nd += 1
                    per[j] += 1
                    op["dsem"] = j
                    op["dval"] = 16 * per[j]
                elif op["sig"]:
                    cnt += 1
                    op["signo"] = cnt
        with contextlib.ExitStack() as es:
            esem = {e: es.enter_context(nc.semaphore("s_" + e)) for e in self.ENGS}
            dsem = {e: [es.enter_context(nc.semaphore("d_%s%d" % (e, j))) for j in range(self.NDMA)]
                    for e in ("sp", "pool", "act")}
            block = es.enter_context(nc.Block())

            def run(ename, eng):
                seen = {}
                for op in ops[ename]:
                    for (e2, i2) in sorted(op["deps"]):
                        o2 = ops[e2][i2]
                        if o2["dma"]:
                            sem, val = dsem[e2][o2["dsem"]], o2["dval"]
                        else:
                            sem, val = esem[e2], o2["signo"]
                        key = id(sem)
                        if seen.get(key, 0) >= val:
                            continue
                        eng.wait_ge(sem, val)
                        seen[key] = val
                    if op["dma"]:
                        sem = dsem[ename][op["dsem"]]
                        prev = op["dval"] - 16
                        if prev > 0 and seen.get(id(sem), 0) < prev:
                            eng.wait_ge(sem, prev)
                            seen[id(sem)] = prev
                        op["fn"](eng).then_inc(sem, 16)
                    else:
                        ins = op["fn"](eng)
                        if op["sig"]:
                            ins.then_inc(esem[ename], 1)
                if ename in dsem:
                    per = {}
                    for op in ops[ename]:
                        if op["dma"]:
                            per[op["dsem"]] = op["dval"]
                    for j, v in per.items():
                        eng.wait_ge(dsem[ename][j], v)

            @block.sync
            def _(e):
                run("sp", e)

            @block.gpsimd
            def _(e):
                run("pool", e)

            @block.scalar
            def _(e):
                run("act", e)

            @block.vector
            def _(e):
                run("dve", e)

            @block.tensor
            def _(e):
                run("pe", e)


class Ctx:
    def __init__(self, nc, sched, es):
        self.nc = nc
        self.s = sched
        self.es = es
        self.n = 0

    @contextlib.contextmanager
    def phase(self):
        old = self.es
        with contextlib.ExitStack() as es:
            self.es = es
            try:
                yield
            finally:
                self.es = old
                self.s.barrier()

    def rot(self, shape, dt, name, n, psum=False):
        name = "%s_r%d" % (name, self.n + 1)
        t = (self.ps if psum else self.sb)([128, n] + list(shape), dt, name)
        return Rot(t, name, n)

    def sb(self, shape, dt, name=None):
        self.n += 1
        return self.es.enter_context(self.nc.sbuf_tensor("%s_%d" % (name or "sb", self.n), list(shape), dt))

    def ps(self, shape, dt, name=None):
        self.n += 1
        return self.es.enter_context(self.nc.psum_tensor("%s_%d" % (name or "ps", self.n), list(shape), dt))

    def dram(self, name, shape, dt):
        return self.nc.dram_tensor(name, list(shape), dt, kind="Internal").ap()


class Rot:
    def __init__(self, t, name, n):
        self.t = t
        self.name = name
        self.n = n
        self.i = -1

    def next(self):
        self.i = (self.i + 1) % self.n
        return self.t[:, self.i], (self.name, self.i)


def load_weight(c, wbuf, wkey, w_dram, K, N, c0=0):
    wv = w_dram.rearrange("(k p) n -> p k n", p=128)
    for k in range(K):
        for n0 in range(0, N, 2048):
            n1 = min(N, n0 + 2048)
            c.s.add("pool", lambda e, k=k, n0=n0, n1=n1: e.dma_start(out=wbuf[:, k, n0:n1], in_=wv[:, k, c0 + n0:c0 + n1]),
                    wp=[wkey], dma=True)


def norm_bufs(c):
    L = {}
    L["junk"] = c.rot([1024], BF16, "junk", 1)
    L["ss"] = c.rot([1], F32, "ss", 4)
    L["rs"] = c.rot([2], F32, "rs", 4)
    L["hb"] = c.rot([1024], BF16, "hb", 2)
    L["tp"] = c.rot([8, 128], BF16, "tp", 2, psum=True)
    return L


def rstd_tile(c, G, L, xt, xkey):
    s = c.s
    junk, jk = L["junk"].next()
    ss, ssk = L["ss"].next()
    s.add("act", lambda e: e.activation(out=junk, in_=xt, func=AF.Square, accum_out=ss[:, 0:1]), r=[xkey], w=[jk, ssk])
    rs, rsk = L["rs"].next()
    s.add("act", lambda e: e.activation(out=rs[:, 0:1], in_=ss[:, 0:1], func=AF.Sqrt, scale=1.0 / D, bias=G["eps"][:, 0:1]),
          r=[ssk, "eps"], w=[rsk])
    s.add("dve", lambda e: e.reciprocal(out=rs[:, 1:2], in_=rs[:, 0:1]), r=[rsk], wp=[rsk])
    return rs, rsk


def norm_tile(c, G, L, xt, xkey, hT, hTkey, col0):
    s = c.s
    rs, rsk = rstd_tile(c, G, L, xt, xkey)
    hb, hbk = L["hb"].next()
    s.add("dve", lambda e: e.scalar_tensor_tensor(out=hb, in0=xt, scalar=rs[:, 1:2], in1=G["gb"][:], op0=ALU.mult, op1=ALU.mult),
          r=[xkey, rsk, "gb"], w=[hbk])
    tp, tpk = L["tp"].next()
    for k in range(D // 128):
        s.add("pe", lambda e, k=k: e.transpose(out=tp[:, k, :], in_=hb[:, k * 128:(k + 1) * 128], identity=G["identb"][:]),
              r=[hbk, "identb"], wp=[tpk])
    s.add("act", lambda e: e.copy(out=hT[:, :, col0:col0 + 128], in_=tp), r=[tpk], wp=[hTkey])


def load_gvec(c, G, gvec_dram):
    c.s.add("sp", lambda e: e.dma_start(out=G["gb"][:], in_=gvec_dram.partition_broadcast(128)), w=["gb"], dma=True)


def phase_mlp1(c, G, x_src, gvec_dram, w1buf, w1key, aT_dram):
    s = c.s
    with c.phase():
        L = norm_bufs(c)
        xts = c.rot([1024], F32, "xt", 3)
        hTs = c.rot([8, 512], BF16, "hT", 2)
        rls = c.rot([512], F32, "rl", 2)
        a4s = c.rot([4, 512], BF16, "a4", 2)
        mms = c.rot([512], F32, "mm", 4, psum=True)
        load_gvec(c, G, gvec_dram)
        aTv = aT_dram.rearrange("(j p) t -> p j t", p=128)
        for tg in range(S // 512):
            hT, hTk = hTs.next()
            for ti in range(4):
                t0 = tg * 512 + ti * 128
                xt, xk = xts.next()
                s.add("sp", lambda e, xt=xt, t0=t0: e.dma_start(out=xt, in_=x_src[t0:t0 + 128, :]), r=[("xs", t0 // 128)], w=[xk], dma=True)
                norm_tile(c, G, L, xt, xk, hT, hTk, ti * 128)
            for j4 in range(DFF // 512):
                a4, a4k = a4s.next()
                for jj in range(4):
                    j = j4 * 4 + jj
                    ps, psk = mms.next()
                    for k in range(8):
                        s.add("pe", lambda e, ps=ps, k=k, j=j, hT=hT: e.matmul(ps, lhsT=w1buf[:, k, j * 128:(j + 1) * 128], rhs=hT[:, k, :],
                                                                             start=(k == 0), stop=(k == 7)),
                              r=[w1key, hTk], wp=[psk])
                    rl, rlk = rls.next()
                    s.add("act", lambda e, ps=ps, rl=rl: e.activation(out=rl, in_=ps, func=AF.Relu), r=[psk], w=[rlk])
                    s.add("pool", lambda e, rl=rl, a4=a4, jj=jj: e.tensor_tensor(out=a4[:, jj, :], in0=rl, in1=rl, op=ALU.mult),
                          r=[rlk], wp=[a4k])
                s.add("pool", lambda e, a4=a4, j4=j4, tg=tg: e.dma_start(out=aTv[:, j4 * 4:(j4 + 1) * 4, tg * 512:(tg + 1) * 512], in_=a4),
                      r=[a4k], wp=[("aT", tg)], dma=True)


def phase_out_proj(c, G, aT_dram, KC, wbuf, wkey, xs_dram, x_src, akey):
    s = c.s
    with c.phase():
        xts = c.rot([1024], F32, "xt", 3)
        ats = c.rot([KC, 256], BF16, "at", 2)
        mms = c.rot([512], F32, "mm", 4, psum=True)
        aTv = aT_dram.rearrange("(j p) t -> p j t", p=128)
        for tg in range(S // 256):
            at, atk = ats.next()
            s.add("sp", lambda e, at=at, tg=tg: e.dma_start(out=at, in_=aTv[:, :, tg * 256:(tg + 1) * 256]),
                  r=[(akey, tg // 2)], w=[atk], dma=True)
            for ti in range(2):
                t0 = tg * 256 + ti * 128
                xt, xk = xts.next()
                s.add("sp", lambda e, xt=xt, t0=t0: e.dma_start(out=xt, in_=x_src[t0:t0 + 128, :]), r=[("xs", t0 // 128)], w=[xk], dma=True)
                for half in range(2):
                    ps, psk = mms.next()
                    for j in range(KC):
                        s.add("pe", lambda e, ps=ps, j=j, at=at, ti=ti, half=half: e.matmul(
                            ps, lhsT=at[:, j, ti * 128:(ti + 1) * 128], rhs=wbuf[:, j, half * 512:(half + 1) * 512],
                            start=(j == 0), stop=(j == KC - 1)), r=[wkey, atk], wp=[psk])
                    s.add("dve", lambda e, ps=ps, xt=xt, half=half: e.tensor_tensor(
                        out=xt[:, half * 512:(half + 1) * 512], in0=ps, in1=xt[:, half * 512:(half + 1) * 512], op=ALU.add),
                        r=[psk, xk], wp=[xk])
                s.add("pool", lambda e, xt=xt, t0=t0: e.dma_start(out=xs_dram[t0:t0 + 128, :], in_=xt), r=[xk], w=[("xs", t0 // 128)], dma=True)


def phase_final_norm(c, G, xs_dram, g_dram, out_dram):
    s = c.s
    with c.phase():
        L = norm_bufs(c)
        xts = c.rot([1024], F32, "xt", 3)
        load_gvec(c, G, g_dram)
        for t in range(NT):
            xt, xk = xts.next()
            s.add("sp", lambda e, xt=xt, t=t: e.dma_start(out=xt, in_=xs_dram[t * 128:(t + 1) * 128, :]), r=[("xs", t)], w=[xk], dma=True)
            rs, rsk = rstd_tile(c, G, L, xt, xk)
            s.add("dve", lambda e, xt=xt, rs=rs: e.scalar_tensor_tensor(out=xt, in0=xt, scalar=rs[:, 1:2], in1=G["gb"][:], op0=ALU.mult, op1=ALU.mult),
                  r=[xk, rsk, "gb"], w=[xk])
            s.add("pool", lambda e, xt=xt, t=t: e.dma_start(out=out_dram[t * 128:(t + 1) * 128, :], in_=xt), r=[xk], w=[("out", t)], dma=True)


DSWA = ((128, 1), (512, 4), (2048, 16))
PADK = 64


def _t5_bucket(rel):
    nb = 16
    max_exact = 8
    ret = np.where(rel > 0, nb, 0)
    n = np.abs(rel)
    nf = np.maximum(n, 1).astype(np.float32)
    large = max_exact + (np.log(nf / np.float32(max_exact)) / np.float32(math.log(1024 / max_exact))
                         * np.float32(nb - max_exact)).astype(np.int32)
    large = np.minimum(large, nb - 1)
    return ret + np.where(n < max_exact, n, large)


def attn_consts():
    cs = {}
    oh = np.zeros((3, 32, 384), np.float32)
    for g, (win, dil) in enumerate(DSWA):
        for j in range(383):
            rel = j - 191
            if abs(rel) <= 64:
                oh[g, int(_t5_bucket(np.array(rel * dil))), j] = 1.0
    cs["c_oh"] = oh
    a = np.arange(128)[:, None]
    b = np.arange(128)[None, :]
    m = np.zeros((128, 2, 128), np.float32)
    m[:, 0, :] = np.where(a >= b, 0.0, NEG)
    m[:, 1, :] = np.where(a <= b, 0.0, NEG)
    cs["c_band"] = m
    return cs


def attn_dram(c):
    A = {}
    for g, (win, d) in enumerate(DSWA):
        L = S // d
        NP = d * (L + 2 * PADK)
        A["qT%d" % g] = c.dram("qT%d" % g, [384, NP], BF16)
        A["kT%d" % g] = c.dram("kT%d" % g, [384, NP], BF16)
        A["v%d" % g] = c.dram("v%d" % g, [NP, 390], BF16)
    A["ou"] = c.dram("ou", [S, 18, 65], F32)
    A["rv"] = c.dram("rv", [3, 18, 384], F32)
    return A


def attn_init(c, G, T, A):
    s = c.s
    with c.phase():
        z = c.sb([128, 4, 512], BF16, "zero")
        s.add("dve", lambda e: e.memset(z[:], 0.0), w=["zero"])
        for g, (win, d) in enumerate(DSWA):
            L = S // d
            NP = d * (L + 2 * PADK)
            kv = A["kT%d" % g].rearrange("(c p) n -> p c n", p=128)
            for cc in range(3):
                for n0 in range(0, NP, 2048):
                    n1 = min(NP, n0 + 2048)
                    s.add("sp", lambda e, kv=kv, cc=cc, n0=n0, n1=n1: e.dma_start(
                        out=kv[:, cc, n0:n1], in_=z[:].rearrange("p a n -> p (a n)")[:, 0:n1 - n0]),
                        r=["zero"], wp=[("kT", g)], dma=True)
            vv = A["v%d" % g].rearrange("(t p) c -> p t c", p=128)
            for t0 in range(0, NP // 128, 4):
                t1 = min(NP // 128, t0 + 4)
                s.add("sp", lambda e, vv=vv, t0=t0, t1=t1: e.dma_start(out=vv[:, t0:t1, :], in_=z[:, 0:t1 - t0, 0:390]),
                      r=["zero"], wp=[("v", g)], dma=True)
        rb = c.sb([32, 18], F32, "rb")
        oh = c.sb([32, 3, 384], F32, "oh")
        rvs = c.sb([18, 3, 384], F32, "rvs")
        pr = c.ps([18, 3, 512], F32, "rvp")
        s.add("sp", lambda e: e.dma_start(out=rb[:], in_=T["rel_bias"]), w=["rb"], dma=True)
        s.add("sp", lambda e: e.dma_start(out=oh[:], in_=T["c_oh"].rearrange("g b j -> b g j")), w=["oh"], dma=True)
        for g in range(3):
            s.add("pe", lambda e, g=g: e.matmul(pr[:, g, 0:384], lhsT=rb[:], rhs=oh[:, g, :], start=True, stop=True),
                  r=["rb", "oh"], wp=["rvp"])
        for g in range(3):
            s.add("act", lambda e, g=g: e.copy(out=rvs[:, g, :], in_=pr[:, g, 0:384]), r=["rvp"], wp=["rvs"])
        s.add("sp", lambda e: e.dma_start(out=A["rv"].rearrange("g h j -> h g j"), in_=rvs[:]), r=["rvs"], w=["rv_d"], dma=True)


def phase_attn_proj(c, G, T, A, x_src, gvec_dram, wbuf, wkey):
    s = c.s
    W = wbuf.rearrange("p a n -> p (a n)")[:, 0:8 * 3456].rearrange("p (k n) -> p k n", n=3456)
    with c.phase():
        Ln = norm_bufs(c)
        xts = c.rot([1024], F32, "xt", 3)
        hTs = c.rot([8, 512], BF16, "hT", 2)
        qks = c.rot([6, 512], BF16, "qk", 2)
        vts = c.rot([6, 65], BF16, "vt", 3)
        mms = c.rot([512], F32, "mm", 4, psum=True)
        s.add("dve", lambda e: e.memset(vts.t[:], 1.0), w=[(vts.name, i) for i in range(3)])
        load_gvec(c, G, gvec_dram)
        for g, (win, d) in enumerate(DSWA):
            L = S // d
            LP = L + 2 * PADK
            xperm = x_src.rearrange("(u r) c -> r u c", r=d)
            qd = A["qT%d" % g].rearrange("(c p) n -> p c n", p=128)
            kd = A["kT%d" % g].rearrange("(c p) n -> p c n", p=128)
            vd = A["v%d" % g]
            for pg in range(S // 512):
                hT, hTk = hTs.next()
                for ti in range(4):
                    p0 = pg * 512 + ti * 128
                    r_, u0 = p0 // L, p0 % L
                    xt, xk = xts.next()
                    s.add("sp", lambda e, xt=xt, xperm=xperm, r_=r_, u0=u0: e.dma_start(out=xt, in_=xperm[r_, u0:u0 + 128, :]),
                          r=[("xs", "all")], w=[xk], dma=True)
                    norm_tile(c, G, Ln, xt, xk, hT, hTk, ti * 128)
                qk, qkk = qks.next()
                for which in range(2):
                    for cc in range(3):
                        ps, psk = mms.next()
                        col = which * 1152 + g * 384 + cc * 128
                        for k in range(8):
                            s.add("pe", lambda e, ps=ps, k=k, col=col, hT=hT: e.matmul(ps, lhsT=W[:, k, col:col + 128], rhs=hT[:, k, :],
                                                                                     start=(k == 0), stop=(k == 7)), r=[wkey, hTk], wp=[psk])
                        s.add("act", lambda e, ps=ps, qk=qk, which=which, cc=cc: e.activation(
                            out=qk[:, which * 3 + cc, :], in_=ps, func=AF.Copy, scale=(0.125 if which == 0 else 1.0)), r=[psk], wp=[qkk])
                run = min(512, L)
                for q0 in range(0, 512, run):
                    p0 = pg * 512 + q0
                    r_, u0 = p0 // L, p0 % L
                    col0 = r_ * LP + PADK + u0
                    s.add("pool", lambda e, qk=qk, qd=qd, col0=col0, q0=q0, run=run: e.dma_start(
                        out=qd[:, :, col0:col0 + run], in_=qk[:, 0:3, q0:q0 + run]), r=[qkk], wp=[("qT", g)], dma=True)
                    s.add("pool", lambda e, qk=qk, kd=kd, col0=col0, q0=q0, run=run: e.dma_start(
                        out=kd[:, :, col0:col0 + run], in_=qk[:, 3:6, q0:q0 + run]), r=[qkk], wp=[("kT", g)], dma=True)
                for ti in range(4):
                    p0 = pg * 512 + ti * 128
                    r_, u0 = p0 // L, p0 % L
                    row0 = r_ * LP + PADK + u0
                    ps, psk = mms.next()
                    colv = 2304 + g * 384
                    for k in range(8):
                        s.add("pe", lambda e, ps=ps, k=k, hT=hT, ti=ti, colv=colv: e.matmul(
                            ps[:, 0:384], lhsT=hT[:, k, ti * 128:(ti + 1) * 128], rhs=W[:, k, colv:colv + 384],
                            start=(k == 0), stop=(k == 7)), r=[wkey, hTk], wp=[psk])
                    vt, vtk = vts.next()
                    s.add("dve", lambda e, ps=ps, vt=vt: e.tensor_copy(out=vt[:, :, 0:64], in_=ps[:, 0:384].rearrange("p (h e) -> p h e", e=64)),
                          r=[psk], wp=[vtk])
                    s.add("pool", lambda e, vt=vt, vd=vd, row0=row0: e.dma_start(
                        out=vd[row0:row0 + 128, :], in_=vt.rearrange("p h e -> p (h e)")), r=[vtk], wp=[("v", g)], dma=True)


def phase_attn_core(c, G, T, A):
    s = c.s
    with c.phase():
        bias = c.sb([128, 6, 2, 128], F32, "bias")
        band = c.sb([128, 2, 128], F32, "band")
        hank = c.sb([128, 6, 2, 128], F32, "hank")
        qts = c.rot([3, 128], BF16, "qt", 2)
        kts = c.rot([3, 256], BF16, "kt", 2)
        vts = c.rot([2, 390], BF16, "vv", 2)
        sbs = c.rot([256], F32, "sbs", 3)
        pts = c.rot([256], BF16, "pts", 3)
        ous = c.rot([6, 65], F32, "ous", 2)
        sts = c.rot([512], F32, "st", 3, psum=True)
        ops_ = c.rot([512], F32, "op", 2, psum=True)
        s.add("sp", lambda e: e.dma_start(out=band[:], in_=T["c_band"]), w=["band"], dma=True)
        ou_d = A["ou"]
        for g, (win, d) in enumerate(DSWA):
            L = S // d
            LP = L + 2 * PADK
            for hg in range(6):
                for kt in range(2):
                    base = 0 if kt == 0 else 128
                    src = bass.AP(tensor=A["rv"].tensor, offset=(g * 18 + g * 6 + hg) * 384 + base, ap=[[1, 128], [1, 128]])
                    s.add("sp", lambda e, hg=hg, kt=kt, src=src: e.dma_start(out=hank[:, hg, kt, :], in_=src),
                          r=["rv_d"], wp=["hank"], dma=True)
            hv = hank[:]
            hrev = bass.AP(tensor=hv.tensor, offset=hv.offset + 127, ap=[list(hv.ap[0]), [256, 6], [128, 2], [-1, 128]])
            s.add("dve", lambda e, hrev=hrev: e.tensor_tensor(out=bias[:], in0=hrev, in1=band[:].unsqueeze(1).to_broadcast([128, 6, 2, 128]), op=ALU.add),
                  r=["hank", "band"], w=["bias"])
            qd = A["qT%d" % g].rearrange("(c p) n -> p c n", p=128)
            kd = A["kT%d" % g].rearrange("(c p) n -> p c n", p=128)
            vd = A["v%d" % g]
            ouv = ou_d.rearrange("(u r) h c -> r u h c", r=d)
            for r_ in range(d):
                for u0 in range(0, L, 128):
                    col0 = r_ * LP + PADK + u0
                    qt, qtk = qts.next()
                    kt_, ktk = kts.next()
                    vv, vvk = vts.next()
                    s.add("sp", lambda e, qt=qt, qd=qd, col0=col0: e.dma_start(out=qt, in_=qd[:, :, col0:col0 + 128]),
                          r=[("qT", g)], w=[qtk], dma=True)
                    s.add("sp", lambda e, kt_=kt_, kd=kd, col0=col0: e.dma_start(out=kt_, in_=kd[:, :, col0 - 64:col0 + 192]),
                          r=[("kT", g)], w=[ktk], dma=True)
                    s.add("sp", lambda e, vv=vv, vd=vd, col0=col0: e.dma_start(
                        out=vv, in_=vd[col0 - 64:col0 + 192, :].rearrange("(t p) c -> p t c", p=128)), r=[("v", g)], w=[vvk], dma=True)
                    op, opk = ops_.next()
                    for hg in range(6):
                        cc, sx = hg // 2, hg % 2
                        st, stk = sts.next()
                        for kt in range(2):
                            s.add("pe", lambda e, st=st, kt_=kt_, qt=qt, cc=cc, sx=sx, kt=kt: e.matmul(
                                st[:, kt * 128:(kt + 1) * 128], lhsT=kt_[sx * 64:(sx + 1) * 64, cc, kt * 128:(kt + 1) * 128],
                                rhs=qt[sx * 64:(sx + 1) * 64, cc, :], start=True, stop=True), r=[qtk, ktk], wp=[stk])
                        sb_, sbk = sbs.next()
                        s.add("dve", lambda e, st=st, sb_=sb_, hg=hg: e.tensor_tensor(
                            out=sb_, in0=st[:, 0:256], in1=bias[:, hg].rearrange("p k n -> p (k n)"), op=ALU.add), r=[stk, "bias"], w=[sbk])
                        pt, ptk = pts.next()
                        s.add("act", lambda e, sb_=sb_, pt=pt: e.activation(out=pt, in_=sb_, func=AF.Exp), r=[sbk], w=[ptk])
                        for kt in range(2):
                            s.add("pe", lambda e, op=op, pt=pt, vv=vv, hg=hg, kt=kt: e.matmul(
                                op[:, hg * 65:(hg + 1) * 65], lhsT=pt[:, kt * 128:(kt + 1) * 128], rhs=vv[:, kt, hg * 65:(hg + 1) * 65],
                                start=(kt == 0), stop=(kt == 1)), r=[ptk, vvk], wp=[opk])
                    ou, ouk = ous.next()
                    s.add("act", lambda e, ou=ou, op=op: e.copy(out=ou.rearrange("p h c -> p (h c)"), in_=op[:, 0:390]), r=[opk], w=[ouk])
                    s.add("pool", lambda e, ou=ou, ouv=ouv, r_=r_, u0=u0, g=g: e.dma_start(
                        out=ouv[r_, u0:u0 + 128, g * 6:(g + 1) * 6, :], in_=ou), r=[ouk], wp=[("ou", "all")], dma=True)


def proj_residual(c, G, L, ob, obk, KC, wbuf, wkey, xt, xk, tps, mms):
    s = c.s
    oT, oTk = L["oT"].next()
    for k0 in range(0, KC, 8):
        k1 = min(KC, k0 + 8)
        tp, tpk = tps.next()
        for k in range(k0, k1):
            s.add("pe", lambda e, k=k, k0=k0, tp=tp: e.transpose(out=tp[:, k - k0, :], in_=ob[:, k * 128:(k + 1) * 128], identity=G["identb"][:]),
                  r=[obk, "identb"], wp=[tpk])
        s.add("act", lambda e, tp=tp, k0=k0, k1=k1: e.copy(out=oT[:, k0:k1, :], in_=tp[:, 0:k1 - k0, :]), r=[tpk], wp=[oTk])
    for half in range(2):
        ps, psk = mms.next()
        for k in range(KC):
            s.add("pe", lambda e, ps=ps, k=k, half=half: e.matmul(ps, lhsT=oT[:, k, :], rhs=wbuf[:, k, half * 512:(half + 1) * 512],
                                                                start=(k == 0), stop=(k == KC - 1)), r=[wkey, oTk], wp=[psk])
        s.add("dve", lambda e, ps=ps, half=half: e.tensor_tensor(
            out=xt[:, half * 512:(half + 1) * 512], in0=ps, in1=xt[:, half * 512:(half + 1) * 512], op=ALU.add), r=[psk, xk], wp=[xk])


def phase_attn_comb(c, G, T, A, x_src, xs_dram, wbuf, wkey):
    s = c.s
    with c.phase():
        xts = c.rot([1024], F32, "xt", 3)
        ous = c.rot([18, 65], F32, "oul", 2)
        zs = c.rot([12], F32, "zs", 2)
        obs = c.rot([1152], BF16, "ob", 2)
        L = {"oT": c.rot([9, 128], BF16, "oT", 2)}
        tps = c.rot([8, 128], BF16, "tp", 2, psum=True)
        mms = c.rot([512], F32, "mm", 4, psum=True)
        for t in range(NT):
            ou, ouk = ous.next()
            s.add("sp", lambda e, ou=ou, t=t: e.dma_start(out=ou, in_=A["ou"][t * 128:(t + 1) * 128]), r=[("ou", "all")], w=[ouk], dma=True)
            xt, xk = xts.next()
            s.add("sp", lambda e, xt=xt, t=t: e.dma_start(out=xt, in_=x_src[t * 128:(t + 1) * 128, :]), r=[("xs", "all")], w=[xk], dma=True)
            z, zk = zs.next()
            s.add("dve", lambda e, z=z, ou=ou: e.tensor_tensor(out=z[:, 0:6], in0=ou[:, 0:6, 64], in1=ou[:, 6:12, 64], op=ALU.add), r=[ouk], wp=[zk])
            s.add("dve", lambda e, z=z, ou=ou: e.tensor_tensor(out=z[:, 0:6], in0=z[:, 0:6], in1=ou[:, 12:18, 64], op=ALU.add), r=[ouk, zk], wp=[zk])
            s.add("dve", lambda e, z=z: e.reciprocal(out=z[:, 6:12], in_=z[:, 0:6]), r=[zk], wp=[zk])
            ob, obk = obs.next()
            ouv = ou.rearrange("p (g h) c -> p g h c", g=3)
            obv = ob.rearrange("p (g h e) -> p g h e", g=3, h=6)
            for hg in range(6):
                s.add("dve", lambda e, hg=hg, ouv=ouv, obv=obv, z=z: e.tensor_scalar(
                    out=obv[:, :, hg, :], in0=ouv[:, :, hg, 0:64], scalar1=z[:, 6 + hg:7 + hg], scalar2=None, op0=ALU.mult), r=[ouk, zk], wp=[obk])
            proj_residual(c, G, L, ob, obk, 9, wbuf, wkey, xt, xk, tps, mms)
            s.add("pool", lambda e, xt=xt, t=t: e.dma_start(out=xs_dram[t * 128:(t + 1) * 128, :], in_=xt), r=[xk], w=[("xs", t)], dma=True)


NH = 8


def gdn_consts():
    cs = {}
    i = np.arange(128)[:, None]
    j = np.arange(128)[None, :]
    same = (i // 64) == (j // 64)
    m = np.zeros((4, 128, 128), np.float32)
    m[0] = (same & (i > j))
    m[1] = (same & (j >= i))
    m[2] = (same & (i < j))
    m[3] = (same & (j <= i))
    cs["c_gmask"] = m
    hm = np.zeros((128, 2), np.float32)
    hm[:64, 0] = 1.0
    hm[64:, 1] = 1.0
    cs["c_half"] = hm
    rs = np.ones((S,), np.float32)
    rs[::64] = 0.0
    cs["c_reset"] = rs
    dm = np.zeros((16, 2), np.float32)
    dm[:8, 0] = 1.0
    dm[8:, 1] = 1.0
    cs["c_dirm"] = dm
    return cs


def gdn_dram(c):
    Gd = {}
    Gd["pT"] = c.dram("g_pT", [3072, S + 4], BF16)
    Gd["zs"] = c.dram("g_zs", [S, 1024], BF16)
    Gd["gpre"] = c.dram("g_gpre", [32, S], F32)
    Gd["GR"] = c.dram("g_GR", [5, 16, S], F32)
    Gd["EGL"] = c.dram("g_EGL", [16, 64], F32)
    Gd["qn"] = c.dram("g_qn", [S, 1024], BF16)
    Gd["kn"] = c.dram("g_kn", [S, 1024], BF16)
    Gd["v"] = c.dram("g_v", [S, 1024], BF16)
    Gd["o0"] = c.dram("g_o0", [S, 1024], F32)
    Gd["o1"] = c.dram("g_o1", [S, 1024], F32)
    return Gd


def gdn_init(c, G, T, Gd):
    s = c.s
    with c.phase():
        z = c.sb([128, 24, 2], BF16, "zpad")
        s.add("dve", lambda e: e.memset(z[:], 0.0), w=["zpad"])
        pv = Gd["pT"].rearrange("(c p) n -> p c n", p=128)
        s.add("sp", lambda e: e.dma_start(out=pv[:, :, 0:2], in_=z[:]), r=["zpad"], wp=["pTpad"], dma=True)
        s.add("sp", lambda e: e.dma_start(out=pv[:, :, S + 2:S + 4], in_=z[:]), r=["zpad"], wp=["pTpad"], dma=True)


def phase_gdn_proj(c, G, T, Gd, x_src, gvec_dram, wbuf, wkey, wS):
    s = c.s
    W = wbuf.rearrange("p (k a) n -> p k (a n)", a=4)
    with c.phase():
        Ln = norm_bufs(c)
        xts = c.rot([1024], F32, "xt", 3)
        hTs = c.rot([8, 512], BF16, "hT", 2)
        p4s = c.rot([4, 512], BF16, "p4", 2)
        zts = c.rot([1024], BF16, "zt", 2)
        gps = c.rot([512], F32, "gp", 2)
        mms = c.rot([512], F32, "mm", 4, psum=True)
        load_gvec(c, G, gvec_dram)
        pv = Gd["pT"].rearrange("(c p) n -> p c n", p=128)
        for tg in range(S // 512):
            hT, hTk = hTs.next()
            for ti in range(4):
                t0 = tg * 512 + ti * 128
                xt, xk = xts.next()
                s.add("sp", lambda e, xt=xt, t0=t0: e.dma_start(out=xt, in_=x_src[t0:t0 + 128, :]), w=[xk], dma=True)
                norm_tile(c, G, Ln, xt, xk, hT, hTk, ti * 128)
            for c4 in range(6):
                p4, p4k = p4s.next()
                for cj in range(4):
                    cc = c4 * 4 + cj
                    ps, psk = mms.next()
                    for k in range(8):
                        s.add("pe", lambda e, ps=ps, k=k, cc=cc, hT=hT: e.matmul(ps, lhsT=W[:, k, cc * 128:(cc + 1) * 128], rhs=hT[:, k, :],
                                                                               start=(k == 0), stop=(k == 7)), r=[wkey, hTk], wp=[psk])
                    s.add("act", lambda e, ps=ps, p4=p4, cj=cj: e.copy(out=p4[:, cj, :], in_=ps), r=[psk], wp=[p4k])
                s.add("pool", lambda e, p4=p4, c4=c4, tg=tg: e.dma_start(out=pv[:, c4 * 4:(c4 + 1) * 4, 2 + tg * 512:2 + (tg + 1) * 512], in_=p4),
                      r=[p4k], wp=["pT"], dma=True)
            for ti in range(4):
                t0 = tg * 512 + ti * 128
                zt, ztk = zts.next()
                for half in range(2):
                    ps, psk = mms.next()
                    for k in range(8):
                        s.add("pe", lambda e, ps=ps, k=k, hT=hT, ti=ti, half=half: e.matmul(
                            ps, lhsT=hT[:, k, ti * 128:(ti + 1) * 128], rhs=W[:, k, 3072 + half * 512:3072 + (half + 1) * 512],
                            start=(k == 0), stop=(k == 7)), r=[wkey, hTk], wp=[psk])
                    s.add("act", lambda e, ps=ps, zt=zt, half=half: e.activation(out=zt[:, half * 512:(half + 1) * 512], in_=ps, func=AF.Silu),
                          r=[psk], wp=[ztk])
                s.add("pool", lambda e, zt=zt, t0=t0: e.dma_start(out=Gd["zs"][t0:t0 + 128, :], in_=zt), r=[ztk], wp=["zs"], dma=True)
            ps, psk = mms.next()
            for k in range(8):
                s.add("pe", lambda e, ps=ps, k=k, hT=hT: e.matmul(ps[0:32, :], lhsT=wS[:, k, :], rhs=hT[:, k, :], start=(k == 0), stop=(k == 7)),
                      r=["wS", hTk], wp=[psk])
            gp, gpk = gps.next()
            s.add("act", lambda e, ps=ps, gp=gp: e.copy(out=gp[0:32, :], in_=ps[0:32, :]), r=[psk], w=[gpk])
            s.add("pool", lambda e, gp=gp, tg=tg: e.dma_start(out=Gd["gpre"][:, tg * 512:(tg + 1) * 512], in_=gp[0:32, :]), r=[gpk], wp=["gpre"], dma=True)


def phase_gdn_gates(c, G, T, Gd, jl):
    s = c.s
    with c.phase():
        dtb = c.sb([32, 1], F32, "dtb")
        nea = c.sb([32, 1], F32, "nea")
        one = c.sb([32, 1], F32, "one1")
        dirm = c.sb([16, 2], F32, "dirm")
        rst = c.sb([16, 1024], F32, "rst")
        egl = c.sb([16, 64], F32, "egl")
        s.add("dve", lambda e: e.memset(dtb[:], 0.0), w=["dtb"])
        s.add("dve", lambda e: e.memset(nea[:], 0.0), w=["nea"])
        s.add("dve", lambda e: e.memset(one[:], 1.0), w=["one1"])
        s.add("sp", lambda e: e.dma_start(out=dtb[0:16, :], in_=T["gdn_dt_bias"][jl].rearrange("d (h o) -> (d h) o", o=1)), w=["dtb"], dma=True)
        s.add("sp", lambda e: e.dma_start(out=nea[0:16, :], in_=T["gdn_a_log"][jl].rearrange("d (h o) -> (d h) o", o=1)), w=["nea"], dma=True)
        s.add("sp", lambda e: e.dma_start(out=dirm[:], in_=T["c_dirm"]), w=["dirm"], dma=True)
        s.add("sp", lambda e: e.dma_start(out=rst[:], in_=T["c_reset"][0:1024].partition_broadcast(16)), w=["rst"], dma=True)
        s.add("act", lambda e: e.activation(out=nea[0:16, :], in_=nea[0:16, :], func=AF.Exp), r=["nea"], w=["nea"])
        s.add("dve", lambda e: e.tensor_scalar(out=nea[0:16, :], in0=nea[0:16, :], scalar1=-1.0, scalar2=None, op0=ALU.mult), r=["nea"], w=["nea"])
        names = ["gp", "xa", "ax", "e1", "sp", "g", "Gc", "Gp", "Gs", "eG", "ek", "nG", "bt"]
        tl = {n: c.sb([32, 1024], F32, "gg_" + n) for n in names}
        for q4 in range(4):
            sl = slice(q4 * 1024, (q4 + 1) * 1024)
            t = tl
            s.add("sp", lambda e, sl=sl: e.dma_start(out=t["gp"][:], in_=Gd["gpre"][:, sl]), r=["gpre"], w=["gp"], dma=True)
            s.add("dve", lambda e: e.tensor_scalar(out=t["xa"][:], in0=t["gp"][:], scalar1=dtb[:, 0:1], scalar2=None, op0=ALU.add), r=["gp", "dtb"], w=["xa"])
            s.add("dve", lambda e: e.scalar_tensor_tensor(out=t["ax"][:], in0=t["xa"][:], scalar=-1.0, in1=t["xa"][:], op0=ALU.mult, op1=ALU.max), r=["xa"], w=["ax"])
            s.add("act", lambda e: e.activation(out=t["e1"][:], in_=t["ax"][:], func=AF.Exp, scale=-1.0), r=["ax"], w=["e1"])
            s.add("act", lambda e: e.activation(out=t["e1"][:], in_=t["e1"][:], func=AF.Ln, bias=one[:, 0:1]), r=["e1", "one1"], w=["e1"])
            s.add("dve", lambda e: e.scalar_tensor_tensor(out=t["sp"][:], in0=t["xa"][:], scalar=0.0, in1=t["e1"][:], op0=ALU.max, op1=ALU.add),
                  r=["xa", "e1"], w=["sp"])
            s.add("dve", lambda e: e.tensor_scalar(out=t["g"][:], in0=t["sp"][:], scalar1=nea[:, 0:1], scalar2=None, op0=ALU.mult), r=["sp", "nea"], w=["g"])
            g16 = t["g"][0:16, :]
            Gc = t["Gc"][0:16, :]
            Gp = t["Gp"][0:16, :]
            Gs = t["Gs"][0:16, :]
            s.add("dve", lambda e: e.tensor_tensor_scan(out=Gc, data0=rst[:], data1=g16, initial=0.0, op0=ALU.mult, op1=ALU.add), r=["g", "rst"], w=["Gc"])
            Gc3 = Gc.rearrange("p (c t) -> p c t", t=64)
            GLb = Gc3[:, :, 63:64].to_broadcast([16, 16, 64])
            s.add("dve", lambda e: e.tensor_tensor(out=Gp.rearrange("p (c t) -> p c t", t=64), in0=GLb, in1=Gc3, op=ALU.subtract), r=["Gc"], w=["Gp"])
            s.add("dve", lambda e: e.tensor_tensor(out=Gp, in0=Gp, in1=g16, op=ALU.add), r=["Gp", "g"], w=["Gp"])
            s.add("dve", lambda e: e.tensor_scalar(out=Gp, in0=Gp, scalar1=dirm[:, 1:2], scalar2=None, op0=ALU.mult), r=["Gp", "dirm"], w=["Gp"])
            s.add("dve", lambda e: e.scalar_tensor_tensor(out=Gs, in0=Gc, scalar=dirm[:, 0:1], in1=Gp, op0=ALU.mult, op1=ALU.add), r=["Gc", "Gp", "dirm"], w=["Gs"])
            s.add("act", lambda e: e.activation(out=t["eG"][0:16, :], in_=Gs, func=AF.Exp), r=["Gs"], w=["eG"])
            s.add("dve", lambda e: e.tensor_tensor(out=t["ek"][0:16, :].rearrange("p (c t) -> p c t", t=64), in0=GLb,
                                                   in1=Gs.rearrange("p (c t) -> p c t", t=64), op=ALU.subtract), r=["Gc", "Gs"], w=["ek"])
            s.add("act", lambda e: e.activation(out=t["ek"][0:16, :], in_=t["ek"][0:16, :], func=AF.Exp), r=["ek"], w=["ek"])
            s.add("dve", lambda e: e.tensor_scalar(out=t["nG"][0:16, :], in0=Gs, scalar1=-1.0, scalar2=None, op0=ALU.mult), r=["Gs"], w=["nG"])
            s.add("act", lambda e: e.activation(out=t["bt"][:], in_=t["gp"][:], func=AF.Sigmoid), r=["gp"], w=["bt"])
            s.add("act", lambda e, q4=q4: e.activation(out=egl[:, q4 * 16:(q4 + 1) * 16], in_=Gc3[:, :, 63], func=AF.Exp), r=["Gc"], wp=["egl"])
            for qi, (nm, lo) in enumerate([("Gs", 0), ("eG", 0), ("ek", 0), ("nG", 0), ("bt", 16)]):
                s.add("pool", lambda e, nm=nm, lo=lo, qi=qi, sl=sl: e.dma_start(out=Gd["GR"][qi, :, sl], in_=t[nm][lo:lo + 16, :]),
                      r=[nm], wp=["GR"], dma=True)
        s.add("pool", lambda e: e.dma_start(out=Gd["EGL"], in_=egl[:]), r=["egl"], w=["EGL"], dma=True)


def phase_gdn_conv(c, G, T, Gd, jl):
    s = c.s
    with c.phase():
        cwrs = c.rot([768], F32, "cwr", 2)
        cw = c.sb([128, 24, 5], F32, "cw")
        Dg = c.sb([128, 24, 5, 128], BF16, "Dg")
        pis = c.rot([4, 516], BF16, "pin", 2)
        svs = c.rot([8, 512], F32, "sv", 1)
        sqs = c.rot([8, 128], F32, "sq", 1)
        sss = c.rot([16], F32, "ssq", 2)
        ots = c.rot([1024], BF16, "ot", 2)
        mms = c.rot([512], F32, "mm", 3, psum=True)
        tfs = c.rot([1024], F32, "tf", 2, psum=True)
        cwp = c.ps([128, 24, 8], F32, "cwp")
        cwd = T["gdn_conv_w"][jl].rearrange("j o n -> j (o n)")
        for q in range(4):
            cwr, cwrk = cwrs.next()
            s.add("sp", lambda e, cwr=cwr, q=q: e.dma_start(out=cwr[0:5, :], in_=cwd[:, q * 768:(q + 1) * 768]), w=[cwrk], dma=True)
            for c6 in range(6):
                cc = q * 6 + c6
                s.add("pe", lambda e, cc=cc, c6=c6, cwr=cwr: e.transpose(out=cwp[:, cc, 0:5], in_=cwr[0:5, c6 * 128:(c6 + 1) * 128], identity=G["identf"][0:5, 0:5]),
                      r=[cwrk, "identf"], wp=["cwp"])
        s.add("act", lambda e: e.copy(out=cw[:], in_=cwp[:, :, 0:5]), r=["cwp"], w=["cw"])
        for cc in range(24):
            for j in range(5):
                s.add("dve", lambda e, cc=cc, j=j: e.tensor_scalar(out=Dg[:, cc, j, :], in0=G["identf"][:], scalar1=cw[:, cc, j:j + 1], scalar2=None, op0=ALU.mult),
                      r=["cw", "identf"], wp=["Dg"])
        pv = Gd["pT"].rearrange("(c p) n -> p c n", p=128)
        for tg in range(S // 512):
            for ty in range(3):
                sv, svk = svs.next()
                for c4 in range(2):
                    pin, pink = pis.next()
                    c0 = ty * 8 + c4 * 4
                    s.add("sp", lambda e, pin=pin, c0=c0, tg=tg: e.dma_start(out=pin, in_=pv[:, c0:c0 + 4, tg * 512:tg * 512 + 516]),
                          r=["pT", "pTpad"], w=[pink], dma=True)
                    for cj in range(4):
                        cc = c0 + cj
                        ps, psk = mms.next()
                        for j in range(5):
                            s.add("pe", lambda e, ps=ps, cc=cc, j=j, pin=pin, cj=cj: e.matmul(ps, lhsT=Dg[:, cc, j, :], rhs=pin[:, cj, j:j + 512],
                                                                                             start=(j == 0), stop=(j == 4)), r=["Dg", pink], wp=[psk])
                        s.add("act", lambda e, ps=ps, sv=sv, ci=c4 * 4 + cj: e.activation(out=sv[:, ci, :], in_=ps, func=AF.Silu), r=[psk], wp=[svk])
                for bi in range(4):
                    t0 = tg * 512 + bi * 128
                    tf, tfk = tfs.next()
                    for hh in range(8):
                        s.add("pe", lambda e, tf=tf, hh=hh, sv=sv, bi=bi: e.transpose(out=tf[:, hh * 128:(hh + 1) * 128], in_=sv[:, hh, bi * 128:(bi + 1) * 128],
                                                                                      identity=G["identf"][:]), r=[svk, "identf"], wp=[tfk])
                    ot, otk = ots.next()
                    if ty == 2:
                        s.add("act", lambda e, ot=ot, tf=tf: e.copy(out=ot, in_=tf), r=[tfk], w=[otk])
                    else:
                        sq, sqk = sqs.next()
                        ss, ssk = sss.next()
                        s.add("act", lambda e, sq=sq, tf=tf: e.activation(out=sq.rearrange("p h e -> p (h e)"), in_=tf, func=AF.Square), r=[tfk], w=[sqk])
                        s.add("dve", lambda e, ss=ss, sq=sq: e.tensor_reduce(out=ss[:, 0:8], in_=sq, op=ALU.add, axis=AX.X), r=[sqk], wp=[ssk])
                        s.add("act", lambda e, ss=ss: e.activation(out=ss[:, 0:8], in_=ss[:, 0:8], func=AF.Sqrt, bias=G["eps"][:, 0:1]), r=[ssk, "eps"], wp=[ssk])
                        s.add("dve", lambda e, ss=ss: e.reciprocal(out=ss[:, 8:16], in_=ss[:, 0:8]), r=[ssk], wp=[ssk])
                        if ty == 0:
                            s.add("dve", lambda e, ss=ss: e.tensor_scalar(out=ss[:, 8:16], in0=ss[:, 8:16], scalar1=128.0 ** -0.5, scalar2=None, op0=ALU.mult),
                                  r=[ssk], wp=[ssk])
                        s.add("dve", lambda e, ot=ot, tf=tf, ss=ss: e.tensor_tensor(
                            out=ot.rearrange("p (h e) -> p h e", e=128), in0=tf.rearrange("p (h e) -> p h e", e=128),
                            in1=ss[:, 8:16].unsqueeze(2).to_broadcast([128, 8, 128]), op=ALU.mult), r=[tfk, ssk], w=[otk])
                    dst = [Gd["qn"], Gd["kn"], Gd["v"]][ty]
                    s.add("pool", lambda e, ot=ot, dst=dst, t0=t0: e.dma_start(out=dst[t0:t0 + 128, :], in_=ot), r=[otk], wp=[("qkv", ty)], dma=True)


def phase_gdn_chunk(c, G, T, Gd, dr):
    s = c.s
    HG = 4
    with c.phase():
        msk = c.sb([128, 2, 128], F32, "gmask")
        half = c.sb([128, 2], F32, "halfm")
        eglb = c.sb([128, 8, 64], F32, "eglb")
        Sf = c.sb([128, 8, 128], F32, "Sf")
        Sb = c.sb([128, 8, 128], BF16, "Sb")
        qns = c.rot([1024], BF16, "qn", 1)
        kns = c.rot([1024], BF16, "kn", 1)
        vs_ = c.rot([1024], BF16, "vv", 1)
        rws = c.rot([128], F32, "rw", 2)
        cls = c.rot([64], F32, "cl", 2)
        gbs = c.rot([8, 128], F32, "gbb", 1)
        kTs = c.rot([8, 128], BF16, "kT", 2)
        qTs = c.rot([8, 128], BF16, "qT", 2)
        ots = c.rot([1024], F32, "ot", 1)
        f32p = {n: c.rot([128], F32, n, 3) for n in ("gc", "ee", "em")}
        As = c.rot([2, 128], F32, "AP", 8)
        XTs = c.rot([128], F32, "XT", 8)
        XTb = c.rot([128], BF16, "XTb", 8)
        b4 = {n: c.rot([128], BF16, n, 4) for n in ("vb", "kbe", "qe")}
        b8 = {n: c.rot([128], BF16, n, 8) for n in ("wT", "iT", "qdT", "kd0", "kd1")}
        us = c.rot([128], F32, "u", 8)
        vns = c.rot([128], BF16, "vn", 4)
        tps = c.rot([8, 128], BF16, "tp", 2, psum=True)
        pms = c.rot([512], F32, "pm", 6, psum=True)
        rtp = pms
        s.add("sp", lambda e: e.dma_start(out=msk[:], in_=T["c_gmask"][2 * dr:2 * dr + 2].rearrange("m i j -> i m j")), w=["gmask"], dma=True)
        s.add("sp", lambda e: e.dma_start(out=half[:], in_=T["c_half"]), w=["halfm"], dma=True)
        s.add("sp", lambda e: e.dma_start(out=eglb[:], in_=Gd["EGL"][dr * 8:(dr + 1) * 8, :].partition_broadcast(128)), r=["EGL"], w=["eglb"], dma=True)
        s.add("dve", lambda e: e.memset(Sf[:], 0.0), w=["Sf"])
        s.add("dve", lambda e: e.memset(Sb[:], 0.0), w=["Sb"])
        s.add("dve", lambda e: e.memset(vns.t[:], 0.0), w=[(vns.name, i) for i in range(4)])
        s.add("dve", lambda e: e.memset(ots.t[:], 0.0), w=[(ots.name, 0)])
        o_d = Gd["o%d" % dr]
        blocks = list(range(NT)) if dr == 0 else list(range(NT - 1, -1, -1))
        blocks = blocks[:DBG_NB]
        corder = (0, 1) if dr == 0 else (1, 0)
        for b in blocks:
            if DBG_SUB <= -3:
                continue
            t0 = b * 128
            qn, qnk = qns.next()
            kn, knk = kns.next()
            vv, vvk = vs_.next()
            rw, rwk = rws.next()
            gbb, gbk = gbs.next()
            s.add("sp", lambda e, qn=qn, t0=t0: e.dma_start(out=qn, in_=Gd["qn"][t0:t0 + 128, :]), r=[("qkv", 0)], w=[qnk], dma=True)
            s.add("sp", lambda e, kn=kn, t0=t0: e.dma_start(out=kn, in_=Gd["kn"][t0:t0 + 128, :]), r=[("qkv", 1)], w=[knk], dma=True)
            s.add("sp", lambda e, vv=vv, t0=t0: e.dma_start(out=vv, in_=Gd["v"][t0:t0 + 128, :]), r=[("qkv", 2)], w=[vvk], dma=True)
            for q in range(5):
                s.add("sp", lambda e, rw=rw, t0=t0, q=q: e.dma_start(out=rw[q * 8:(q + 1) * 8, :], in_=Gd["GR"][q, dr * 8:(dr + 1) * 8, t0:t0 + 128]),
                      r=["GR"], wp=[rwk], dma=True)
            s.add("sp", lambda e, gbb=gbb, t0=t0: e.dma_start(out=gbb, in_=Gd["GR"][0, dr * 8:(dr + 1) * 8, t0:t0 + 128].partition_broadcast(128)),
                  r=["GR"], w=[gbk], dma=True)
            if DBG_SUB <= -2:
                continue
            rp, rpk = rtp.next()
            s.add("pe", lambda e, rp=rp, rw=rw: e.transpose(out=rp[:, 0:40], in_=rw[0:40, :], identity=G["identf"][0:40, 0:40]), r=[rwk, "identf"], wp=[rpk])
            cl, clk = cls.next()
            s.add("act", lambda e, cl=cl, rp=rp: e.copy(out=cl[:, 0:40], in_=rp[:, 0:40]), r=[rpk], wp=[clk])
            s.add("dve", lambda e, cl=cl: e.tensor_tensor(out=cl[:, 40:48], in0=cl[:, 32:40], in1=cl[:, 8:16], op=ALU.mult), r=[clk], wp=[clk])
            s.add("dve", lambda e, cl=cl: e.tensor_scalar(out=cl[:, 48:56], in0=cl[:, 16:24], scalar1=half[:, 0:1], scalar2=None, op0=ALU.mult), r=[clk, "halfm"], wp=[clk])
            s.add("dve", lambda e, cl=cl: e.tensor_scalar(out=cl[:, 56:64], in0=cl[:, 16:24], scalar1=half[:, 1:2], scalar2=None, op0=ALU.mult), r=[clk, "halfm"], wp=[clk])
            if DBG_SUB <= -1:
                continue
            kT, kTk = kTs.next()
            qT, qTk = qTs.next()
            for (src, srck, dst, dstk) in ((kn, knk, kT, kTk), (qn, qnk, qT, qTk)):
                tp, tpk = tps.next()
                for h in range(8):
                    s.add("pe", lambda e, tp=tp, h=h, src=src: e.transpose(out=tp[:, h, :], in_=src[:, h * 128:(h + 1) * 128], identity=G["identb"][:]),
                          r=[srck, "identb"], wp=[tpk])
                s.add("act", lambda e, dst=dst, tp=tp: e.copy(out=dst, in_=tp), r=[tpk], w=[dstk])
            ot, otk = ots.next()
            for h0 in range(0, NH, HG):
                hs = list(range(h0, h0 + HG))
                R = {}
                if DBG_SUB < 1:
                    continue
                for h in hs:
                    d = R[h] = {}
                    kk, kkk = pms.next()
                    s.add("pe", lambda e, kk=kk, h=h, kT=kT: e.matmul(kk[:, 0:128], lhsT=kT[:, h, :], rhs=kT[:, h, :], start=True, stop=True), r=[kTk], wp=[kkk])
                    gc, gck = f32p["gc"].next()
                    s.add("dve", lambda e, gc=gc, h=h, gbb=gbb, cl=cl: e.tensor_scalar(out=gc, in0=gbb[:, h, :], scalar1=cl[:, h:h + 1], scalar2=None, op0=ALU.max),
                          r=[gbk, clk], w=[gck])
                    ee, eek = f32p["ee"].next()
                    s.add("act", lambda e, ee=ee, gc=gc, h=h, cl=cl: e.activation(out=ee, in_=gc, func=AF.Exp, scale=-1.0, bias=cl[:, h:h + 1]), r=[gck, clk], w=[eek])
                    em, emk = f32p["em"].next()
                    s.add("dve", lambda e, em=em, ee=ee: e.tensor_tensor(out=em, in0=ee, in1=msk[:, 0, :], op=ALU.mult), r=[eek, "gmask"], w=[emk])
                    ap_, apk = As.next()
                    s.add("dve", lambda e, em=em, h=h, cl=cl: e.tensor_scalar(out=em, in0=em, scalar1=cl[:, 32 + h:33 + h], scalar2=None, op0=ALU.mult), r=[emk, clk], w=[emk])
                    s.add("dve", lambda e, ap_=ap_, kk=kk, em=em: e.tensor_tensor(out=ap_[:, 0, :], in0=kk[:, 0:128], in1=em, op=ALU.mult), r=[kkk, emk], wp=[apk])
                    tq, tqk = pms.next()
                    s.add("pe", lambda e, tq=tq, ap_=ap_: e.transpose(out=tq[:, 0:128], in_=ap_[:, 0, :], identity=G["identf"][:]), r=[apk, "identf"], wp=[tqk])
                    s.add("act", lambda e, tq=tq, ap_=ap_: e.copy(out=ap_[:, 1, :], in_=tq[:, 0:128]), r=[tqk], wp=[apk])
                    xt_, xtk = XTs.next()
                    s.add("dve", lambda e, xt_=xt_, ap_=ap_: e.tensor_tensor(out=xt_, in0=G["identf"][:], in1=ap_[:, 1, :], op=ALU.subtract), r=[apk, "identf"], w=[xtk])
                    d["ap"], d["apk"], d["xt"], d["xtk"] = ap_, apk, xt_, xtk
                for st_ in range(5 if DBG_SUB >= 2 else 0):
                    for h in hs:
                        d = R[h]
                        ap_, apk, xt_, xtk = d["ap"], d["apk"], d["xt"], d["xtk"]
                        pp, ppk = pms.next()
                        s.add("pe", lambda e, pp=pp, ap_=ap_: e.matmul(pp[:, 0:128], lhsT=ap_[:, 1, :], rhs=ap_[:, 0, :], start=True, stop=True), r=[apk], wp=[ppk])
                        s.add("pe", lambda e, pp=pp, ap_=ap_: e.matmul(pp[:, 128:256], lhsT=ap_[:, 0, :], rhs=ap_[:, 1, :], start=True, stop=True), r=[apk], wp=[ppk])
                        an, ank = As.next()
                        s.add("act", lambda e, an=an, pp=pp: e.copy(out=an.rearrange("p a n -> p (a n)"), in_=pp[:, 0:256]), r=[ppk], w=[ank])
                        xu, xuk = pms.next()
                        s.add("pe", lambda e, xu=xu, xt_=xt_: e.matmul(xu[:, 0:128], lhsT=G["identf"][:], rhs=xt_, start=True, stop=False), r=[xtk, "identf"], wp=[xuk])
                        s.add("pe", lambda e, xu=xu, xt_=xt_, an=an: e.matmul(xu[:, 0:128], lhsT=an[:, 0, :], rhs=xt_, start=False, stop=True), r=[xtk, ank], wp=[xuk])
                        xn, xnk = (XTs.next() if st_ < 4 else XTb.next())
                        s.add("dve", lambda e, xn=xn, xu=xu: e.tensor_copy(out=xn, in_=xu[:, 0:128]), r=[xuk], w=[xnk])
                        d["ap"], d["apk"], d["xt"], d["xtk"] = an, ank, xn, xnk
                for h in (hs if DBG_SUB >= 3 else []):
                    d = R[h]
                    xt_, xtk = d["xt"], d["xtk"]
                    hsl = slice(h * 128, (h + 1) * 128)
                    vb, vbk = b4["vb"].next()
                    s.add("dve", lambda e, vb=vb, vv=vv, hsl=hsl, h=h, cl=cl: e.tensor_scalar(out=vb, in0=vv[:, hsl], scalar1=cl[:, 32 + h:33 + h], scalar2=None, op0=ALU.mult),
                          r=[vvk, clk], w=[vbk])
                    kbe, kbek = b4["kbe"].next()
                    s.add("dve", lambda e, kbe=kbe, kn=kn, hsl=hsl, h=h, cl=cl: e.tensor_scalar(out=kbe, in0=kn[:, hsl], scalar1=cl[:, 40 + h:41 + h], scalar2=None, op0=ALU.mult),
                          r=[knk, clk], w=[kbek])
                    pu, puk = pms.next()
                    s.add("pe", lambda e, pu=pu, xt_=xt_, vb=vb: e.matmul(pu[:, 0:128], lhsT=xt_, rhs=vb, start=True, stop=True), r=[xtk, vbk], wp=[puk])
                    s.add("pe", lambda e, pu=pu, xt_=xt_, kbe=kbe: e.matmul(pu[:, 128:256], lhsT=kbe, rhs=xt_, start=True, stop=True), r=[xtk, kbek], wp=[puk])
                    u, uk = us.next()
                    wT, wTk = b8["wT"].next()
                    s.add("act", lambda e, u=u, pu=pu: e.copy(out=u, in_=pu[:, 0:128]), r=[puk], w=[uk])
                    s.add("act", lambda e, wT=wT, pu=pu: e.activation(out=wT, in_=pu[:, 128:256], func=AF.Copy, scale=-1.0), r=[puk], w=[wTk])
                    qk_, qkk = pms.next()
                    s.add("pe", lambda e, qk_=qk_, h=h, kT=kT, qT=qT: e.matmul(qk_[:, 0:128], lhsT=kT[:, h, :], rhs=qT[:, h, :], start=True, stop=True), r=[kTk, qTk], wp=[qkk])
                    gc, gck = f32p["gc"].next()
                    s.add("dve", lambda e, gc=gc, h=h, gbb=gbb, cl=cl: e.tensor_scalar(out=gc, in0=gbb[:, h, :], scalar1=cl[:, h:h + 1], scalar2=None, op0=ALU.min),
                          r=[gbk, clk], w=[gck])
                    ee, eek = f32p["ee"].next()
                    s.add("act", lambda e, ee=ee, gc=gc, h=h, cl=cl: e.activation(out=ee, in_=gc, func=AF.Exp, scale=1.0, bias=cl[:, 24 + h:25 + h]), r=[gck, clk], w=[eek])
                    em, emk = f32p["em"].next()
                    s.add("dve", lambda e, em=em, ee=ee: e.tensor_tensor(out=em, in0=ee, in1=msk[:, 1, :], op=ALU.mult), r=[eek, "gmask"], w=[emk])
                    iT, iTk = b8["iT"].next()
                    s.add("dve", lambda e, iT=iT, qk_=qk_, em=em: e.tensor_tensor(out=iT, in0=qk_[:, 0:128], in1=em, op=ALU.mult), r=[qkk, emk], w=[iTk])
                    qe, qek = b4["qe"].next()
                    s.add("dve", lambda e, qe=qe, qn=qn, hsl=hsl, h=h, cl=cl: e.tensor_scalar(out=qe, in0=qn[:, hsl], scalar1=cl[:, 8 + h:9 + h], scalar2=None, op0=ALU.mult),
                          r=[qnk, clk], w=[qek])
                    tq, tqk = tps.next()
                    s.add("pe", lambda e, tq=tq, qe=qe: e.transpose(out=tq[:, 0, :], in_=qe, identity=G["identb"][:]), r=[qek, "identb"], wp=[tqk])
                    qdT, qdTk = b8["qdT"].next()
                    s.add("act", lambda e, qdT=qdT, tq=tq: e.copy(out=qdT, in_=tq[:, 0, :]), r=[tqk], w=[qdTk])
                    kd0, kd0k = b8["kd0"].next()
                    kd1, kd1k = b8["kd1"].next()
                    s.add("dve", lambda e, kd0=kd0, kn=kn, hsl=hsl, h=h, cl=cl: e.tensor_scalar(out=kd0, in0=kn[:, hsl], scalar1=cl[:, 48 + h:49 + h], scalar2=None, op0=ALU.mult),
                          r=[knk, clk], w=[kd0k])
                    s.add("dve", lambda e, kd1=kd1, kn=kn, hsl=hsl, h=h, cl=cl: e.tensor_scalar(out=kd1, in0=kn[:, hsl], scalar1=cl[:, 56 + h:57 + h], scalar2=None, op0=ALU.mult),
                          r=[knk, clk], w=[kd1k])
                    d.update(u=u, uk=uk, wT=wT, wTk=wTk, iT=iT, iTk=iTk, qdT=qdT, qdTk=qdTk, kd=(kd0, kd1), kdk=(kd0k, kd1k))
                for ci in (corder if DBG_SUB >= 4 else ()):
                    rows = slice(ci * 64, (ci + 1) * 64)
                    chunk = b * 2 + ci
                    for h in hs:
                        d = R[h]
                        p1, p1k = pms.next()
                        s.add("pe", lambda e, p1=p1, d=d, h=h: e.matmul(p1[:, 0:128], lhsT=d["wT"], rhs=Sb[:, h, :], start=True, stop=True), r=[d["wTk"], ("Sb", h)], wp=[p1k])
                        vn, vnk = vns.next()
                        s.add("dve", lambda e, vn=vn, p1=p1, d=d: e.tensor_tensor(out=vn, in0=p1[:, 0:128], in1=d["u"], op=ALU.add),
                              r=[p1k, d["uk"]], w=[vnk])
                        po, pok = pms.next()
                        s.add("pe", lambda e, po=po, d=d, h=h: e.matmul(po[:, 0:128], lhsT=d["qdT"], rhs=Sb[:, h, :], start=True, stop=False), r=[d["qdTk"], ("Sb", h)], wp=[pok])
                        s.add("pe", lambda e, po=po, d=d, vn=vn: e.matmul(po[:, 0:128], lhsT=d["iT"], rhs=vn, start=False, stop=True), r=[d["iTk"], vnk], wp=[pok])
                        s.add("pe", lambda e, po=po, d=d, vn=vn, ci=ci: e.matmul(po[:, 128:256], lhsT=d["kd"][ci], rhs=vn, start=True, stop=True), r=[d["kdk"][ci], vnk], wp=[pok])
                        osl = ot[:, h * 128:(h + 1) * 128]
                        if ci == corder[0]:
                            s.add("dve", lambda e, po=po, osl=osl, ci=ci: e.tensor_scalar(out=osl, in0=po[:, 0:128], scalar1=half[:, ci:ci + 1], scalar2=None, op0=ALU.mult),
                                  r=[pok, "halfm"], wp=[otk])
                        else:
                            ob_, obk_ = f32p["gc"].next()
                            s.add("dve", lambda e, po=po, ob_=ob_, ci=ci: e.tensor_scalar(out=ob_, in0=po[:, 0:128], scalar1=half[:, ci:ci + 1], scalar2=None, op0=ALU.mult),
                                  r=[pok, "halfm"], w=[obk_])
                            s.add("dve", lambda e, ob_=ob_, osl=osl: e.tensor_tensor(out=osl, in0=osl, in1=ob_, op=ALU.add), r=[obk_, otk], wp=[otk])
                        s.add("dve", lambda e, h=h, chunk=chunk: e.tensor_scalar(out=Sf[:, h, :], in0=Sf[:, h, :], scalar1=eglb[:, h, chunk:chunk + 1], scalar2=None, op0=ALU.mult),
                              r=[("Sf", h), "eglb"], w=[("Sf", h)])
                        s.add("dve", lambda e, po=po, h=h: e.tensor_tensor(out=Sf[:, h, :], in0=po[:, 128:256], in1=Sf[:, h, :], op=ALU.add),
                              r=[pok, ("Sf", h)], w=[("Sf", h)])
                        s.add("act", lambda e, h=h: e.copy(out=Sb[:, h, :], in_=Sf[:, h, :]), r=[("Sf", h)], w=[("Sb", h)])
            s.add("pool", lambda e, ot=ot, t0=t0: e.dma_start(out=o_d[t0:t0 + 128, :], in_=ot), r=[otk], wp=["o_d"], dma=True)


def phase_gdn_out(c, G, T, Gd, jl, x_src, xs_dram, wbuf, wkey):
    s = c.s
    with c.phase():
        xts = c.rot([1024], F32, "xt", 2)
        o0s = c.rot([1024], F32, "o0", 2)
        o1s = c.rot([1024], F32, "o1", 2)
        zss = c.rot([1024], BF16, "zz", 2)
        sqs = c.rot([1024], F32, "sq", 1)
        sss = c.rot([16], F32, "ssq", 2)
        obs = c.rot([1024], BF16, "ob", 2)
        nwb = c.sb([128, 128], F32, "nwb")
        L = {"oT": c.rot([8, 128], BF16, "oT", 2)}
        tps = c.rot([8, 128], BF16, "tp", 2, psum=True)
        mms = c.rot([512], F32, "mm", 4, psum=True)
        s.add("sp", lambda e: e.dma_start(out=nwb[:], in_=T["gdn_norm_w"][jl].partition_broadcast(128)), w=["nwb"], dma=True)
        for t in range(NT):
            sl = slice(t * 128, (t + 1) * 128)
            xt, xk = xts.next()
            o0, o0k = o0s.next()
            o1, o1k = o1s.next()
            zz, zzk = zss.next()
            s.add("sp", lambda e, xt=xt, sl=sl: e.dma_start(out=xt, in_=x_src[sl, :]), w=[xk], dma=True)
            s.add("sp", lambda e, o0=o0, sl=sl: e.dma_start(out=o0, in_=Gd["o0"][sl, :]), w=[o0k], dma=True)
            s.add("sp", lambda e, o1=o1, sl=sl: e.dma_start(out=o1, in_=Gd["o1"][sl, :]), w=[o1k], dma=True)
            s.add("sp", lambda e, zz=zz, sl=sl: e.dma_start(out=zz, in_=Gd["zs"][sl, :]), w=[zzk], dma=True)
            s.add("dve", lambda e, o0=o0, o1=o1: e.tensor_tensor(out=o0, in0=o0, in1=o1, op=ALU.add), r=[o0k, o1k], w=[o0k])
            sq, sqk = sqs.next()
            ss, ssk = sss.next()
            s.add("act", lambda e, sq=sq, o0=o0: e.activation(out=sq, in_=o0, func=AF.Square), r=[o0k], w=[sqk])
            s.add("dve", lambda e, ss=ss, sq=sq: e.tensor_reduce(out=ss[:, 0:8], in_=sq.rearrange("p (h e) -> p h e", e=128), op=ALU.add, axis=AX.X), r=[sqk], wp=[ssk])
            s.add("act", lambda e, ss=ss: e.activation(out=ss[:, 0:8], in_=ss[:, 0:8], func=AF.Sqrt, scale=1.0 / 128, bias=G["eps"][:, 0:1]), r=[ssk, "eps"], wp=[ssk])
            s.add("dve", lambda e, ss=ss: e.reciprocal(out=ss[:, 8:16], in_=ss[:, 0:8]), r=[ssk], wp=[ssk])
            o3 = o0.rearrange("p (h e) -> p h e", e=128)
            s.add("dve", lambda e, o3=o3, ss=ss: e.tensor_tensor(out=o3, in0=o3, in1=ss[:, 8:16].unsqueeze(2).to_broadcast([128, 8, 128]), op=ALU.mult), r=[o0k, ssk], w=[o0k])
            s.add("dve", lambda e, o3=o3: e.tensor_tensor(out=o3, in0=o3, in1=nwb[:].unsqueeze(1).to_broadcast([128, 8, 128]), op=ALU.mult), r=[o0k, "nwb"], w=[o0k])
            ob, obk = obs.next()
            s.add("dve", lambda e, ob=ob, o0=o0, zz=zz: e.tensor_tensor(out=ob, in0=o0, in1=zz, op=ALU.mult), r=[o0k, zzk], w=[obk])
            proj_residual(c, G, L, ob, obk, 8, wbuf, wkey, xt, xk, tps, mms)
            s.add("pool", lambda e, xt=xt, sl=sl: e.dma_start(out=xs_dram[sl, :], in_=xt), r=[xk], w=[("xs", t)], dma=True)


IN_SPECS = [
    ("x", [S, D]), ("norm_mix", [4, D]), ("norm_mlp", [4, D]), ("norm_final", [D]), ("rel_bias", [32, 18]),
    ("gdn_w_in", [2, D, 4128]), ("gdn_conv_w", [2, 5, 1, 3072]), ("gdn_a_log", [2, 2, 8]), ("gdn_dt_bias", [2, 2, 8]),
    ("gdn_norm_w", [2, 128]), ("gdn_w_out", [2, 1024, 1024]), ("dswa_w_in", [2, D, 3456]), ("dswa_w_out", [2, 1152, D]),
    ("mlp_w1", [4, D, DFF]), ("mlp_w2", [4, DFF, D]),
]


def host_consts():
    cs = {}
    cs["c_identb"] = np.eye(128, dtype=np.float32).astype(ml_dtypes.bfloat16)
    cs["c_identf"] = np.eye(128, dtype=np.float32)
    cs.update(attn_consts())
    cs.update(gdn_consts())
    return cs


def build_program(layers=(0, 1, 2, 3), do_mixer=True, do_mlp=True, do_final=True):
    nc = bass.Bass("TRN2", target_bir_lowering=False)
    T = {}
    for name, shape in IN_SPECS:
        T[name] = nc.dram_tensor(name, shape, F32, kind="ExternalInput").ap()
    hc = host_consts()
    for name, arr in hc.items():
        T[name] = nc.dram_tensor(name, list(arr.shape), BF16 if arr.dtype == ml_dtypes.bfloat16 else F32, kind="ExternalInput").ap()
    out = nc.dram_tensor("out", [S, D], F32, kind="ExternalOutput").ap()
    sched = Sched(nc)
    with contextlib.ExitStack() as es:
        c = Ctx(nc, sched, es)
        xs = c.dram("xs", [S, D], F32)
        aT = c.dram("aT", [DFF, S], BF16)
        G = {}
        wA = c.sb([128, 32, 1024], BF16, "wA")
        wB = c.sb([128, 32, 1024], BF16, "wB")
        G["gb"] = c.sb([128, 1024], F32, "gb")
        G["eps"] = c.sb([128, 1], F32, "eps")
        G["identb"] = c.sb([128, 128], BF16, "identb")
        G["identf"] = c.sb([128, 128], F32, "identf")
        sched.add("sp", lambda e: e.dma_start(out=G["identb"][:], in_=T["c_identb"]), w=["identb"], dma=True)
        sched.add("sp", lambda e: e.dma_start(out=G["identf"][:], in_=T["c_identf"]), w=["identf"], dma=True)
        sched.add("dve", lambda e: e.memset(G["eps"][:], EPS), w=["eps"])
        wbufs = [(wA, "wA"), (wB, "wB")]
        has_attn = do_mixer and any(li % 2 == 1 for li in layers)
        has_gdn = do_mixer and any(li % 2 == 0 for li in layers)
        A = attn_dram(c) if has_attn else None
        if has_attn:
            attn_init(c, G, T, A)
        Gd = gdn_dram(c) if has_gdn else None
        wS = c.sb([128, 8, 32], BF16, "wS")
        if has_gdn:
            gdn_init(c, G, T, Gd)
        steps = []
        st = {"x": T["x"]}

        def xsrc():
            return st["x"]

        for li in layers:
            jj = li // 2
            if do_mixer and li % 2 == 0:
                def ld1(wb, wk, jj=jj):
                    load_weight(c, wb.rearrange("p (k a) n -> p k (a n)", a=4), wk, T["gdn_w_in"][jj], 8, 4096)
                    load_weight(c, wS, "wS", T["gdn_w_in"][jj], 8, 32, c0=4096)

                def b1(wb, wk, li=li, jj=jj):
                    phase_gdn_proj(c, G, T, Gd, xsrc(), T["norm_mix"][li], wb, wk, wS)
                    if DBG_STOP >= 2:
                        phase_gdn_gates(c, G, T, Gd, jj)
                    if DBG_STOP >= 3:
                        phase_gdn_conv(c, G, T, Gd, jj)

                def ld2(wb, wk, jj=jj):
                    load_weight(c, wb, wk, T["gdn_w_out"][jj], 8, D)

                def b2(wb, wk, jj=jj):
                    if DBG_STOP >= 4:
                        phase_gdn_chunk(c, G, T, Gd, 0)
                    if DBG_STOP >= 5:
                        phase_gdn_chunk(c, G, T, Gd, 1)
                    if DBG_STOP >= 6:
                        phase_gdn_out(c, G, T, Gd, jj, xsrc(), xs, wb, wk)
                        st["x"] = xs
                steps += [(ld1, b1), (ld2, b2)]
            if do_mixer and li % 2 == 1:
                def ld1(wb, wk, jj=jj):
                    Wv = wb.rearrange("p a n -> p (a n)")[:, 0:8 * 3456].rearrange("p (k n) -> p k n", n=3456)
                    load_weight(c, Wv, wk, T["dswa_w_in"][jj], 8, 3456)

                def b1(wb, wk, li=li):
                    phase_attn_proj(c, G, T, A, xsrc(), T["norm_mix"][li], wb, wk)

                def ld2(wb, wk, jj=jj):
                    load_weight(c, wb, wk, T["dswa_w_out"][jj], 9, D)

                def b2(wb, wk):
                    phase_attn_core(c, G, T, A)
                    phase_attn_comb(c, G, T, A, xsrc(), xs, wb, wk)
                    st["x"] = xs
                steps += [(ld1, b1), (ld2, b2)]
            if do_mlp:
                def ld1(wb, wk, li=li):
                    load_weight(c, wb.rearrange("p (k a) n -> p k (a n)", a=4), wk, T["mlp_w1"][li], 8, DFF)

                def b1(wb, wk, li=li):
                    phase_mlp1(c, G, xsrc(), T["norm_mlp"][li], wb.rearrange("p (k a) n -> p k (a n)", a=4), wk, aT)

                def ld2(wb, wk, li=li):
                    load_weight(c, wb, wk, T["mlp_w2"][li], 32, D)

                def b2(wb, wk):
                    phase_out_proj(c, G, aT, 32, wb, wk, xs, xsrc(), "aT")
                    st["x"] = xs
                steps += [(ld1, b1), (ld2, b2)]
        for i in range(min(2, len(steps))):
            steps[i][0](*wbufs[i % 2])
        for i, (ld, body) in enumerate(steps):
            body(*wbufs[i % 2])
            if i + 2 < len(steps):
                steps[i + 2][0](*wbufs[i % 2])
        if do_final:
            phase_final_norm(c, G, xsrc(), T["norm_final"], out)
        sched.emit()
    return nc, hc


_CACHE = {}


def kernel(**inputs):
    if "nc" not in _CACHE:
        _CACHE["nc"] = build_program()
    nc, hc = _CACHE["nc"]
    x = np.ascontiguousarray(np.asarray(inputs["x"], dtype=np.float32))
    B = x.shape[0]
    in_maps = []
    for b in range(B):
        m = {"x": x[b]}
        for name, shape in IN_SPECS[1:]:
            m[name] = np.ascontiguousarray(np.asarray(inputs[name], dtype=np.float32)).reshape(shape)
        m.update(hc)
        in_maps.append(m)
    res = run_bass_kernel_spmd(nc, in_maps, core_ids=list(range(B)))
    return np.stack([np.asarray(r["out"], dtype=np.float32) for r in res.results], axis=0)
```
